# Optimizing a Trainium2 kernel written in Bass

```python
import jax, jax.numpy as jnp
from jax import lax
import numpy as np

D_MODEL = 1024
BATCH = 1
SEQ = 16384
DEPTH = 1

HEAD_DIM = 64
SB_HEADS = 8
SB_WIDTH = SB_HEADS * HEAD_DIM
CONV_GROUPS = 4
CONV_WIDTH_CH = CONV_GROUPS * HEAD_DIM
MEM_HEADS = 4
MEM_WIDTH = MEM_HEADS * HEAD_DIM
MIX_WIDTH = SB_WIDTH + CONV_WIDTH_CH + MEM_WIDTH
N_MEM = 256
CONV_K = 3
SB_BLOCK = 128
EPS = 1e-6

SPLIT_POINTS = [
    SB_WIDTH,
    2 * SB_WIDTH,
    3 * SB_WIDTH,
    3 * SB_WIDTH + CONV_WIDTH_CH,
    3 * SB_WIDTH + 2 * CONV_WIDTH_CH,
    3 * SB_WIDTH + 3 * CONV_WIDTH_CH,
    3 * SB_WIDTH + 3 * CONV_WIDTH_CH + MEM_WIDTH,
]
PROJ_WIDTH = 3 * SB_WIDTH + 3 * CONV_WIDTH_CH + MEM_WIDTH + MIX_WIDTH

kernel_name = "hymba_sbattn_shortconv_memxattn"


def _rmsnorm(x, g):
    xf = x.astype(jnp.float32)
    y = xf * lax.rsqrt(jnp.mean(xf * xf, axis=-1, keepdims=True) + EPS)
    return (y * g.astype(jnp.float32)).astype(x.dtype)


def _stick_breaking_attention(q, k, v):
    b, t, h, dh = q.shape
    nblk = t // SB_BLOCK
    scale = dh ** -0.5
    key_pos = jnp.arange(t)
    q_blocks = q.reshape(b, nblk, SB_BLOCK, h, dh).transpose(1, 0, 2, 3, 4)
    starts = jnp.arange(nblk) * SB_BLOCK

    def block(args):
        q_blk, start = args
        q_pos = start + jnp.arange(SB_BLOCK)
        z = jnp.einsum('bqhd,bkhd->bhqk', q_blk, k,
                       preferred_element_type=jnp.float32) * scale
        before = key_pos[None, :] < q_pos[:, None]
        log_not = jnp.where(before, jax.nn.log_sigmoid(-z), 0.0)
        suffix = lax.cumsum(log_not, axis=3, reverse=True) - log_not
        w = jnp.where(before, jnp.exp(jax.nn.log_sigmoid(z) + suffix), 0.0)
        return jnp.einsum('bhqk,bkhd->bqhd', w.astype(v.dtype), v)

    out = lax.map(block, (q_blocks, starts))
    return out.transpose(1, 0, 2, 3, 4).reshape(b, t, h, dh)


def _causal_depthwise_conv(u, w, bias):
    c = u.shape[-1]
    y = lax.conv_general_dilated(
        u, w[:, None, :].astype(u.dtype), window_strides=(1,),
        padding=[(CONV_K - 1, 0)], dimension_numbers=('NWC', 'WIO', 'NWC'),
        feature_group_count=c)
    return y + bias.astype(u.dtype)


def _memory_attention(q, mk, mv):
    s = jnp.einsum('bthd,bmhd->bhtm', q, mk,
                   preferred_element_type=jnp.float32) * (q.shape[-1] ** -0.5)
    p = jax.nn.softmax(s, axis=-1)
    return jnp.einsum('bhtm,bmhd->bthd', p.astype(mv.dtype), mv)


def setup_inputs(seed: int = 0) -> dict:
    key = jax.random.key(seed)
    ks = jax.random.split(key, 14)
    f32 = jnp.float32
    x = jax.random.normal(ks[0], (BATCH, SEQ, D_MODEL), f32)
    mem = jax.random.normal(ks[1], (BATCH, N_MEM, D_MODEL), f32)
    g_in = 1.0 + 0.02 * jax.random.normal(ks[2], (DEPTH, D_MODEL), f32)
    w_in = jax.random.normal(ks[3], (DEPTH, D_MODEL, PROJ_WIDTH), f32) * D_MODEL ** -0.5
    conv_w = jax.random.normal(ks[4], (DEPTH, CONV_K, CONV_WIDTH_CH), f32) * CONV_K ** -0.5
    conv_b = 0.01 * jax.random.normal(ks[5], (DEPTH, CONV_WIDTH_CH), f32)
    g_mem = 1.0 + 0.02 * jax.random.normal(ks[6], (DEPTH, D_MODEL), f32)
    w_mem_kv = jax.random.normal(ks[7], (DEPTH, D_MODEL, 2 * MEM_WIDTH), f32) * D_MODEL ** -0.5
    g_sb_out = 1.0 + 0.02 * jax.random.normal(ks[8], (DEPTH, SB_WIDTH), f32)
    g_conv_out = 1.0 + 0.02 * jax.random.normal(ks[9], (DEPTH, CONV_WIDTH_CH), f32)
    g_mem_out = 1.0 + 0.02 * jax.random.normal(ks[10], (DEPTH, MEM_WIDTH), f32)
    w_out = jax.random.normal(ks[11], (DEPTH, MIX_WIDTH, D_MODEL), f32) * MIX_WIDTH ** -0.5
    g_final = 1.0 + 0.02 * jax.random.normal(ks[12], (D_MODEL,), f32)
    return {"x": x, "mem": mem, "g_in": g_in, "w_in": w_in, "conv_w": conv_w,
            "conv_b": conv_b, "g_mem": g_mem, "w_mem_kv": w_mem_kv,
            "g_sb_out": g_sb_out, "g_conv_out": g_conv_out, "g_mem_out": g_mem_out,
            "w_out": w_out, "g_final": g_final}


def reference(x, mem, g_in, w_in, conv_w, conv_b, g_mem, w_mem_kv,
              g_sb_out, g_conv_out, g_mem_out, w_out, g_final):
    b, t, _ = x.shape
    for l in range(DEPTH):
        h = _rmsnorm(x, g_in[l])
        proj = h @ w_in[l]
        q_sb, k_sb, v_sb, u_c, b_c, c_c, q_mem, gate = jnp.split(proj, SPLIT_POINTS, axis=-1)

        y_sb = _stick_breaking_attention(
            q_sb.reshape(b, t, SB_HEADS, HEAD_DIM),
            k_sb.reshape(b, t, SB_HEADS, HEAD_DIM),
            v_sb.reshape(b, t, SB_HEADS, HEAD_DIM)).reshape(b, t, SB_WIDTH)

        y_conv = b_c * _causal_depthwise_conv(c_c * u_c, conv_w[l], conv_b[l])

        m = _rmsnorm(mem, g_mem[l])
        mk, mv = jnp.split(m @ w_mem_kv[l], 2, axis=-1)
        nm = mem.shape[1]
        y_mem = _memory_attention(
            q_mem.reshape(b, t, MEM_HEADS, HEAD_DIM),
            mk.reshape(b, nm, MEM_HEADS, HEAD_DIM),
            mv.reshape(b, nm, MEM_HEADS, HEAD_DIM)).reshape(b, t, MEM_WIDTH)

        y = jnp.concatenate([_rmsnorm(y_sb, g_sb_out[l]),
                             _rmsnorm(y_conv, g_conv_out[l]),
                             _rmsnorm(y_mem, g_mem_out[l])], axis=-1)
        x = x + (y * jax.nn.silu(gate)) @ w_out[l]
    return _rmsnorm(x, g_final)
```

```python
import numpy as np
import ml_dtypes
import concourse.bass as bass
import concourse.mybir as mybir
from concourse.bass_utils import run_bass_kernel_spmd

F32 = mybir.dt.float32
BF16 = mybir.dt.bfloat16
AF = mybir.ActivationFunctionType
ALU = mybir.AluOpType
AX = mybir.AxisListType

D = 1024
NCORES = 8
EPS = 1e-6
NEG = -30000.0
CH = 512


class Buf:
    def __init__(self, ap=None):
        self.ap = ap
        self.rd = {}
        self.wr = {}


class Sched:
    def __init__(self):
        self.ops = []
        self.cnt = {}
        self.base = {}

    def op(self, eng, fn, reads=(), writes=(), sem=None, inc=1):
        sm = sem or eng
        d = dict(self.base)

        def add(td):
            for k, v in td.items():
                if d.get(k, 0) < v:
                    d[k] = v

        for b in reads:
            add(b.wr)
        for b in writes:
            add(b.wr)
            add(b.rd)
        self.cnt[sm] = self.cnt.get(sm, 0) + inc
        c = self.cnt[sm]
        for b in reads:
            b.rd[sm] = c
        for b in writes:
            b.wr[sm] = c
            b.rd = {}
        self.ops.append((eng, fn, d, sm, inc))
        return (sm, c)

    def seq(self, eng, fns, reads=(), writes=()):
        tb = Buf()
        for f in fns:
            t = self.op(eng, f, reads=list(reads), writes=list(writes) + [tb])
        return t

    def barrier(self):
        self.base = dict(self.cnt)

    def sem_names(self):
        return sorted(self.cnt.keys())


def bc_last(ap, n):
    return bass.AP(tensor=ap.tensor, offset=ap.offset, ap=[list(a) for a in ap.ap] + [[0, n]])


def build(T, stop=None):
    marks = {}
    NT = T // 128
    TO = T // NCORES
    NTO = TO // 128
    NG1 = T // 512
    NG3 = TO // 512
    assert T % (512 * NCORES) == 0

    nc = bass.Bass("TRN2", target_bir_lowering=False)
    dt = nc.dram_tensor
    x_d = dt("x", [T, D], F32, kind="ExternalInput").ap()
    xo_d = dt("xo", [128 + TO, D], F32, kind="ExternalInput").ap()
    wqkv_d = dt("wqkv", [D, 192], F32, kind="ExternalInput").ap()
    wrest_d = dt("wrest", [D, 2048], F32, kind="ExternalInput").ap()
    wout_d = dt("wout", [D, D], F32, kind="ExternalInput").ap()
    wmem_d = dt("wmem", [D, 512], F32, kind="ExternalInput").ap()
    mem_d = dt("mem", [256, D], F32, kind="ExternalInput").ap()
    vecs_d = dt("vecs", [128, 32], F32, kind="ExternalInput").ap()
    gfin_d = dt("gfin", [128, D], F32, kind="ExternalInput").ap()
    cbf_d = dt("cbf", [128, 384], BF16, kind="ExternalInput").ap()
    out_d = dt("out", [TO, D], F32, kind="ExternalOutput").ap()
    yloc_d = dt("yloc", [64, T], BF16, kind="Internal").ap()
    ycat_d = dt("ycat", [512, T], BF16, kind="Internal").ap()

    S = Sched()
    ARENA_W = 49 * 1024

    import contextlib
    with contextlib.ExitStack() as es:
        arena = es.enter_context(nc.sbuf_tensor("arena", [128, ARENA_W], F32))
        cbf = es.enter_context(nc.sbuf_tensor("cbf_sb", [128, 384], BF16))
        vecs = es.enter_context(nc.sbuf_tensor("vecs_sb", [128, 32], F32))
        zer = es.enter_context(nc.sbuf_tensor("zer", [128, 512], F32))
        cst = es.enter_context(nc.sbuf_tensor("cst", [128, 16], F32))
        onesb = es.enter_context(nc.sbuf_tensor("onesb", [128, 2], BF16))
        stat = es.enter_context(nc.sbuf_tensor("stat", [128, 64], F32))
        banks = [es.enter_context(nc.psum_tensor(f"pb{i}", [128, 512], F32)) for i in range(8)]
        PB = [Buf(b) for b in banks]

        ident = cbf[:, 0:128]
        mneg = cbf[:, 128:256]
        mask01 = cbf[:, 256:384]
        B_cbf, B_vecs, B_zer, B_cst, B_ones = Buf(), Buf(), Buf(), Buf(), Buf()
        gin = vecs[:, 0:8]
        gmem = vecs[:, 8:16]
        gcat = vecs[:, 16:24]
        convw = vecs[:, 24:30]
        convb = vecs[:, 30:32]

        state = {"off": 0}

        def alloc(shape_free, dtype):
            nb = int(np.prod(shape_free)) * (2 if dtype == BF16 else 4)
            nb4 = (nb + 3) // 4
            o = state["off"]
            assert o + nb4 <= ARENA_W, f"arena overflow {o + nb4} > {ARENA_W}"
            state["off"] = o + nb4
            ap = arena[:, o:o + nb4]
            if dtype == BF16:
                ap = ap.bitcast(BF16)
                n = int(np.prod(shape_free))
                ap = ap[:, 0:n]
            if len(shape_free) == 2:
                ap = ap.rearrange("p (a b) -> p a b", a=shape_free[0])
            elif len(shape_free) == 3:
                ap = ap.rearrange("p (a b c) -> p a b c", a=shape_free[0], b=shape_free[1])
            return ap

        S.op("sync", lambda e: e.dma_start(out=cbf[:], in_=cbf_d[:, :]), writes=[B_cbf], sem="c0", inc=16)
        S.op("sync", lambda e: e.dma_start(out=vecs[:], in_=vecs_d[:, :]), writes=[B_vecs], sem="c1", inc=16)
        S.op("dve", lambda e: e.memset(zer[:], 0.0), writes=[B_zer])

        def _cst(e):
            e.memset(cst[:, 0:1], -0.5)
            e.memset(cst[:, 1:2], EPS)
            e.memset(cst[:, 4:8], -1.0)
            e.memset(cst[:, 8:14], -0.5)
            e.memset(cst[0:64, 2:3], 1.0)
            e.memset(cst[64:128, 2:3], 0.0)
            e.memset(cst[0:64, 3:4], 0.0)
            e.memset(cst[64:128, 3:4], 1.0)
            return e.memset(onesb[:], 1.0)
        S.op("pool", _cst, writes=[B_cst, B_ones])

        def make_norm_pipe():
            xt = [alloc([D], F32) for _ in range(3)]
            junk = alloc([D], BF16)
            xn = [alloc([D], BF16) for _ in range(2)]
            st = {"k": 0}
            bxt = [Buf() for _ in range(3)]
            bjunk = Buf()
            bxn = [Buf() for _ in range(2)]
            bst = [Buf() for _ in range(4)]

            def run(src_rows, dst, dst_buf, evac_eng, keep_x=None):
                k = st["k"]
                st["k"] += 1
                a, b2, s4 = k % 3, k % 2, k % 4
                xta, xnb = xt[a], xn[b2]
                ss = stat[:, 4 * s4:4 * s4 + 1]
                ms = stat[:, 4 * s4 + 1:4 * s4 + 2]
                rs = stat[:, 4 * s4 + 2:4 * s4 + 3]
                S.op("sync", lambda e: e.dma_start(out=xta, in_=src_rows), writes=[bxt[a]], sem=f"xld{a}", inc=16)
                S.op("act", lambda e: e.activation(out=junk, in_=xta, func=AF.Square, accum_out=ss),
                     reads=[bxt[a]], writes=[bjunk, bst[s4]])

                S.seq("pool", [
                    lambda e: e.tensor_scalar(out=ms, in0=ss, scalar1=1.0 / D, scalar2=EPS, op0=ALU.mult, op1=ALU.add),
                    lambda e: e.tensor_tensor(out=rs, in0=ms, in1=cst[:, 0:1], op=ALU.pow)],
                    reads=[B_cst], writes=[bst[s4]])
                S.op("dve", lambda e: e.tensor_scalar(out=xnb, in0=xta, scalar1=rs, scalar2=None, op0=ALU.mult),
                     reads=[bxt[a], bst[s4]], writes=[bxn[b2]])
                pbk = PB[6 + b2]
                pv = pbk.ap[:, :].bitcast(BF16).rearrange("p (j t) -> p j t", j=8)

                def _tp(e):
                    for j in range(8):
                        i = e.transpose(out=pv[:, j, :], in_=xnb[:, j * 128:(j + 1) * 128], identity=ident)
                    return i
                S.op("pe", _tp, reads=[bxn[b2], B_cbf], writes=[pbk])
                if evac_eng == "act":
                    S.op("act", lambda e: e.activation(out=dst, in_=pv, func=AF.Copy), reads=[pbk], writes=[dst_buf])
                else:
                    S.op("dve", lambda e: e.tensor_copy(out=dst, in_=pv), reads=[pbk], writes=[dst_buf])
                return xta, bxt[a]
            return run

        mark0 = state["off"]
        Qt = alloc([T], BF16)
        Kt = alloc([T], BF16)
        Vt = alloc([T + 2], BF16)
        Wtok = alloc([NT, 128], BF16)
        B_Qt, B_Kt, B_Vt, B_W = Buf(), Buf(), Buf(), Buf()
        mark_work = state["off"]

        wst = alloc([8, 192], F32)
        wq = alloc([8, 128], BF16)
        wk = alloc([8, 128], BF16)
        wv = alloc([8, 128], BF16)
        B_wst, B_wqkv = Buf(), Buf()
        S.op("sync", lambda e: e.dma_start(out=wst, in_=wqkv_d.rearrange("(j p) c -> p j c", p=128)),
             writes=[B_wst], sem="c2", inc=16)

        def _wzero(e):
            e.memset(wq, 0.0)
            e.memset(wk, 0.0)
            e.memset(wv, 0.0)
            return e.memset(Vt, 0.0)

        def _wprep(e):
            g64 = bc_last(gin, 64)
            e.tensor_tensor(out=wq[:, :, 0:64], in0=wst[:, :, 0:64], in1=g64, op=ALU.mult)
            e.tensor_tensor(out=wk[:, :, 0:64], in0=wst[:, :, 64:128], in1=g64, op=ALU.mult)
            return e.tensor_tensor(out=wv[:, :, 0:64], in0=wst[:, :, 128:192], in1=g64, op=ALU.mult)
        S.seq("dve", [_wzero, _wprep], reads=[B_wst, B_vecs], writes=[B_wqkv, B_Vt])

        norm1 = make_norm_pipe()
        hTg = [alloc([8, 512], BF16) for _ in range(2)]
        B_hTg = [Buf() for _ in range(2)]
        wTg = [alloc([512], BF16) for _ in range(2)]
        B_wTg = [Buf() for _ in range(2)]

        for g in range(NG1):
            hb = hTg[g % 2]
            bh = B_hTg[g % 2]
            for r in range(4):
                i = g * 4 + r
                norm1(x_d[i * 128:(i + 1) * 128, :], hb[:, :, r * 128:(r + 1) * 128], bh,
                      "act" if (i % 2 == 0) else "dve")
            c0 = g * 512

            def proj(w_sb, pbk, hb=hb):
                def f(e):
                    for j in range(8):
                        i_ = e.matmul(pbk.ap[:, :], lhsT=w_sb[:, j, :], rhs=hb[:, j, :], start=(j == 0), stop=(j == 7))
                    return i_
                return f
            S.op("pe", proj(wq, PB[0]), reads=[bh, B_wqkv], writes=[PB[0]])
            S.op("act", lambda e, c0=c0: e.activation(out=Qt[:, c0:c0 + 512], in_=PB[0].ap[:, :], func=AF.Copy),
                 reads=[PB[0]], writes=[B_Qt])
            S.op("pe", proj(wk, PB[1]), reads=[bh, B_wqkv], writes=[PB[1]])
            S.op("dve", lambda e, c0=c0: e.tensor_copy(out=Kt[:, c0:c0 + 512], in_=PB[1].ap[:, :]),
                 reads=[PB[1]], writes=[B_Kt])
            S.op("pe", proj(wv, PB[2]), reads=[bh, B_wqkv], writes=[PB[2]])
            S.op("act", lambda e, c0=c0: e.activation(out=Vt[:, 1 + c0:1 + c0 + 512], in_=PB[2].ap[:, :], func=AF.Copy),
                 reads=[PB[2]], writes=[B_Vt])
            wt = wTg[g % 2]
            bw = B_wTg[g % 2]
            S.op("dve", lambda e, c0=c0, wt=wt: e.tensor_tensor(out=wt, in0=Vt[:, c0:c0 + 512],
                                                                in1=Vt[:, c0 + 1:c0 + 513], op=ALU.subtract),
                 reads=[B_Vt], writes=[bw])
            pw = PB[3].ap[:, 0:256].bitcast(BF16).rearrange("p (a b) -> p a b", a=4)

            def _wtp(e, wt=wt, pw=pw):
                for r in range(4):
                    i_ = e.transpose(out=pw[:, r, :], in_=wt[:, r * 128:(r + 1) * 128], identity=ident)
                return i_
            S.op("pe", _wtp, reads=[bw, B_cbf], writes=[PB[3]])
            S.op("dve", lambda e, g=g, pw=pw: e.tensor_copy(out=Wtok[:, g * 4:(g + 1) * 4, :], in_=pw),
                 reads=[PB[3]], writes=[B_W])

        marks['p1'] = len(S.ops)
        S.barrier()
        state["off"] = mark_work
        NZ, NNB, NPIN, NPT, NPTS = 3, 3, 3, 2, 4
        nb_sb = [alloc([CH], F32) for _ in range(NNB)]
        pin = [alloc([CH], BF16) for _ in range(NPIN)]
        pts = [alloc([CH], BF16) for _ in range(NPTS)]
        ybuf = [alloc([128], BF16) for _ in range(2)]
        B_nb = [Buf() for _ in range(NNB)]
        B_pin = [Buf() for _ in range(NPIN)]
        B_pts = [Buf() for _ in range(NPTS)]
        B_yb = [Buf() for _ in range(2)]
        ZB = [PB[0], PB[1], PB[2]]
        PTB = [PB[3], PB[4]]
        ACC = [PB[5], PB[6]]

        chunks = []
        for i in range(NT):
            e_ = (i + 1) * 128
            j = 0
            while e_ - CH * j > 0:
                b = e_ - CH * j
                a = max(0, b - CH)
                chunks.append([i, j, a, b, j == 0, a == 0])
                j += 1
        NCk = len(chunks)

        def qk(n):
            i, j, a, b, first, last = chunks[n]
            sz = b - a
            zb = ZB[n % NZ]
            t0 = i * 128
            qT = Qt[:, t0:t0 + 128]

            def f(e):
                if first:
                    if sz > 128:
                        e.matmul(zb.ap[:, 0:sz - 128], lhsT=qT, rhs=Kt[:, a:b - 128], start=True, stop=True)
                    e.matmul(zb.ap[:, sz - 128:sz], lhsT=qT, rhs=Kt[:, b - 128:b], start=True, stop=False)
                    return e.matmul(zb.ap[:, sz - 128:sz], lhsT=ident, rhs=mneg, start=False, stop=True)
                return e.matmul(zb.ap[:, 0:sz], lhsT=qT, rhs=Kt[:, a:b], start=True, stop=True)
            S.op("pe", f, reads=[B_Qt, B_Kt, B_cbf], writes=[zb])

        def sig(n):
            i, j, a, b, first, last = chunks[n]
            sz = b - a
            zb = ZB[n % NZ]
            o = nb_sb[n % NNB]
            S.op("act", lambda e: e.activation(out=o[:, 0:sz], in_=zb.ap[:, 0:sz], func=AF.Sigmoid, scale=-0.125),
                 reads=[zb], writes=[B_nb[n % NNB]])

        def scan(n):
            i, j, a, b, first, last = chunks[n]
            sz = b - a
            src = nb_sb[n % NNB]
            dst = pin[n % NPIN]
            prev = pin[(n - 1) % NPIN]

            def f(e):
                init = 1.0 if first else prev[:, 0:1]
                i_ = e.tensor_tensor_scan(out=dst[:, 0:sz][:, ::-1], data0=src[:, 0:sz][:, ::-1], data1=zer[:, 0:sz],
                                          initial=init, op0=ALU.mult, op1=ALU.add)
                return i_
            rd = [B_nb[n % NNB], B_zer] + ([] if first else [B_pin[(n - 1) % NPIN]])
            S.op("dve", f, reads=rd, writes=[B_pin[n % NPIN]])
            if first:
                S.op("dve", lambda e: e.tensor_tensor(out=dst[:, sz - 128:sz], in0=dst[:, sz - 128:sz], in1=mask01,
                                                      op=ALU.mult), reads=[B_cbf], writes=[B_pin[n % NPIN]])

        def tp(n):
            i, j, a, b, first, last = chunks[n]
            sz = b - a
            src = pin[n % NPIN]
            pb = PTB[n % NPT]
            pv = pb.ap[:, :].bitcast(BF16)

            def f(e):
                for blk in range(sz // 128):
                    i_ = e.transpose(out=pv[:, blk * 128:(blk + 1) * 128], in_=src[:, blk * 128:(blk + 1) * 128],
                                     identity=ident)
                return i_
            S.op("pe", f, reads=[B_pin[n % NPIN], B_cbf], writes=[pb])

        def evac(n):
            i, j, a, b, first, last = chunks[n]
            sz = b - a
            pb = PTB[n % NPT]
            pv = pb.ap[:, :].bitcast(BF16)
            o = pts[n % NPTS]
            S.op("act", lambda e: e.activation(out=o[:, 0:sz], in_=pv[:, 0:sz], func=AF.Copy),
                 reads=[pb], writes=[B_pts[n % NPTS]])

        def av(n):
            i, j, a, b, first, last = chunks[n]
            sz = b - a
            src = pts[n % NPTS]
            acc = ACC[i % 2]
            nblk = sz // 128

            def f(e):
                for blk in range(nblk):
                    kt = a // 128 + blk
                    i_ = e.matmul(acc.ap[:, 0:128], lhsT=Wtok[:, kt, :], rhs=src[:, blk * 128:(blk + 1) * 128],
                                  start=(first and blk == 0), stop=(last and blk == nblk - 1))
                return i_
            S.op("pe", f, reads=[B_pts[n % NPTS], B_W], writes=[acc])
            if last:
                t0 = i * 128
                yb = ybuf[i % 2]
                S.op("dve", lambda e: e.tensor_tensor(out=yb[0:64, :], in0=acc.ap[0:64, 0:128], in1=Vt[0:64, t0:t0 + 128],
                                                      op=ALU.add), reads=[acc, B_Vt], writes=[B_yb[i % 2]])
                S.op("pool", lambda e: e.dma_start(out=yloc_d[:, t0:t0 + 128], in_=yb[0:64, :]),
                     reads=[B_yb[i % 2]], sem=f"yst{i % 2}", inc=16, writes=[])

        for n in range(-2, NCk + 2):
            if 0 <= n + 2 < NCk:
                qk(n + 2)
            if 0 <= n + 1 < NCk:
                sig(n + 1)
            if 0 <= n < NCk:
                scan(n)
                tp(n)
            if 0 <= n - 1 < NCk:
                evac(n - 1)
            if 0 <= n - 2 < NCk:
                av(n - 2)

        marks['p2'] = len(S.ops)
        S.barrier()
        B_ycat = Buf()
        S.op("pool", lambda e: e.collective_compute("AllGather", ALU.bypass, replica_groups=[list(range(NCORES))],
                                                    ins=[yloc_d[:, :]], outs=[ycat_d[:, :]]),
             writes=[B_ycat], sem="cc", inc=1)
        S.barrier()

        marks['ag'] = len(S.ops)
        state["off"] = mark0
        Wr = alloc([8, 2048], BF16)
        Wo = alloc([8, D], BF16)
        Wm = alloc([8, 512], BF16)
        B_Wr, B_Wo, B_Wm = Buf(), Buf(), Buf()
        stg = [alloc([1024], F32) for _ in range(2)]
        B_stg = [Buf() for _ in range(2)]
        gfin = alloc([D], F32)
        B_gfin = Buf()
        S.op("sync", lambda e: e.dma_start(out=gfin, in_=gfin_d[:, :]), writes=[B_gfin], sem="c3", inc=16)
        k = 0
        for (src_d, ncol, dstw, gv, bdst) in ((wrest_d, 2048, Wr, gin, B_Wr), (wout_d, D, Wo, gcat, B_Wo),
                                              (wmem_d, 512, Wm, gmem, B_Wm)):
            for j in range(8):
                for c0_ in range(0, ncol, 1024):
                    w_ = min(1024, ncol - c0_)
                    sg_, bs_ = stg[k % 2], B_stg[k % 2]
                    S.op("sync", (lambda sg_=sg_, j=j, src_d=src_d, c0_=c0_, w_=w_:
                                  lambda e: e.dma_start(out=sg_[:, 0:w_], in_=src_d[j * 128:(j + 1) * 128, c0_:c0_ + w_]))(),
                         writes=[bs_], sem=f"stg{k % 2}", inc=16)
                    eng = "dve" if k % 2 == 0 else "pool"
                    S.op(eng, (lambda sg_=sg_, j=j, dstw=dstw, gv=gv, c0_=c0_, w_=w_:
                               lambda e: e.tensor_scalar(out=dstw[:, j, c0_:c0_ + w_], in0=sg_[:, 0:w_],
                                                         scalar1=gv[:, j:j + 1], scalar2=None, op0=ALU.mult))(),
                         reads=[bs_, B_vecs], writes=[bdst])
                    k += 1

        marks['p3w'] = len(S.ops)
        norm3 = make_norm_pipe()
        memT = alloc([8, 256], BF16)
        B_memT = Buf()
        for r in range(2):
            norm3(mem_d[r * 128:(r + 1) * 128, :], memT[:, :, r * 128:(r + 1) * 128], B_memT, "dve")
        mkT = alloc([2, 256], BF16)
        mvh = [alloc([2, 128], BF16) for _ in range(4)]
        B_mkT, B_mv = Buf(), Buf()

        def _mvz(e):
            for h in range(4):
                i_ = e.memset(mvh[h], 0.0)
            return i_
        S.op("dve", _mvz, writes=[B_mv])
        for c in range(2):
            def f(e, c=c):
                for j in range(8):
                    i_ = e.matmul(PB[0].ap[:, 0:256], lhsT=Wm[:, j, c * 128:(c + 1) * 128], rhs=memT[:, j, :],
                                  start=(j == 0), stop=(j == 7))
                return i_
            S.op("pe", f, reads=[B_Wm, B_memT], writes=[PB[0]])
            S.op("dve", (lambda c=c: lambda e: e.tensor_copy(out=mkT[:, c, :], in_=PB[0].ap[:, 0:256]))(),
                 reads=[PB[0]], writes=[B_mkT])
        for mc in range(2):
            def f(e, mc=mc):
                for j in range(8):
                    i_ = e.matmul(PB[1].ap[:, 0:256], lhsT=memT[:, j, mc * 128:(mc + 1) * 128], rhs=Wm[:, j, 256:512],
                                  start=(j == 0), stop=(j == 7))
                return i_
            S.op("pe", f, reads=[B_Wm, B_memT], writes=[PB[1]])
            def fmv(e, mc=mc):
                for h in range(4):
                    o0 = (h % 2) * 64
                    i_ = e.tensor_copy(out=mvh[h][:, mc, o0:o0 + 64], in_=PB[1].ap[:, h * 64:(h + 1) * 64])
                return i_
            S.op("dve", fmv, reads=[PB[1]], writes=[B_mv])

        marks['p3m'] = len(S.ops)
        hTo = [alloc([8, 512], BF16) for _ in range(2)]
        B_hTo = [Buf() for _ in range(2)]
        cu = alloc([2, 514], F32)
        B_cu = Buf()
        u_sb = alloc([2, 512], F32)
        b_sb = alloc([2, 512], F32)
        B_u, B_b = Buf(), Buf()
        sg = alloc([8, 512], BF16)
        B_sg = Buf()
        qmh = [alloc([512], BF16) for _ in range(4)]
        B_qm = Buf()
        t1 = alloc([2, 512], F32)
        B_t1 = Buf()
        yc = alloc([2, 512], F32)
        B_yc = Buf()
        ysb = alloc([4, 512], BF16)
        B_ysb = Buf()
        ysg = alloc([8, 512], BF16)
        B_ysg = Buf()
        ysq = alloc([8, 512], BF16)
        B_ysq = Buf()
        eT = alloc([8, 512], BF16)
        B_eT = [Buf() for _ in range(8)]
        rsb = alloc([512], F32)
        B_rsb = Buf()
        rD = alloc([512], F32)
        B_rD = Buf()
        cneg = alloc([2, 512], F32)
        onesh = alloc([3, 128], BF16)
        B_cn = Buf()
        xres = [alloc([D], F32) for _ in range(2)]
        B_xres = [Buf() for _ in range(2)]
        ob = [alloc([D], F32) for _ in range(2)]
        B_ob = [Buf() for _ in range(2)]
        junk3 = alloc([D], BF16)
        B_junk3 = Buf()
        st3 = alloc([40], F32)
        B_st3 = Buf()

        def _cn(e):
            e.memset(cneg[:, 0, :], -0.5)
            e.memset(cneg[:, 1, :], -1.0)
            e.memset(onesh[:, 0, 0:64], 1.0)
            e.memset(onesh[:, 0, 64:128], 0.0)
            e.memset(onesh[:, 1, 0:64], 0.0)
            e.memset(onesh[:, 1, 64:128], 1.0)
            return e.memset(onesh[:, 2, :], 1.0)
        S.op("dve", _cn, writes=[B_cn])

        norm3(xo_d[0:128, :], hTo[0][:, :, 0:128], B_hTo[0], "dve")
        hb0 = hTo[0]
        for ch in range(2):
            def fu(e, ch=ch):
                for j in range(8):
                    i_ = e.matmul(PB[0].ap[:, 0:128], lhsT=Wr[:, j, ch * 128:(ch + 1) * 128], rhs=hb0[:, j, 0:128],
                                  start=(j == 0), stop=(j == 7))
                return i_
            S.op("pe", fu, reads=[B_Wr, B_hTo[0]], writes=[PB[0]])
            S.op("act", (lambda ch=ch: lambda e: e.activation(out=u_sb[:, ch, 0:128], in_=PB[0].ap[:, 0:128],
                                                               func=AF.Copy))(), reads=[PB[0]], writes=[B_u])

            def fc(e, ch=ch):
                for j in range(8):
                    i_ = e.matmul(PB[1].ap[:, 0:128], lhsT=Wr[:, j, 512 + ch * 128:512 + (ch + 1) * 128],
                                  rhs=hb0[:, j, 0:128], start=(j == 0), stop=(j == 7))
                return i_
            S.op("pe", fc, reads=[B_Wr, B_hTo[0]], writes=[PB[1]])
            S.op("dve", (lambda ch=ch: lambda e: e.tensor_tensor(out=cu[:, ch, 0:2], in0=PB[1].ap[:, 126:128],
                                                                  in1=u_sb[:, ch, 126:128], op=ALU.mult))(),
                 reads=[PB[1], B_u], writes=[B_cu])

        marks['p3h'] = len(S.ops)
        for g in range(NG3):
            hb = hTo[(g + 1) % 2]
            bh = B_hTo[(g + 1) % 2]
            for r in range(4):
                i = g * 4 + r
                norm3(xo_d[128 + i * 128:128 + (i + 1) * 128, :], hb[:, :, r * 128:(r + 1) * 128], bh,
                      "act" if (i % 2 == 0) else "dve")

            def fm_proj(col0, pbk, n=512, hb=hb):
                def f(e):
                    for j in range(8):
                        i_ = e.matmul(pbk.ap[:, 0:n], lhsT=Wr[:, j, col0:col0 + 128], rhs=hb[:, j, 0:n],
                                      start=(j == 0), stop=(j == 7))
                    return i_
                return f
            for c in range(8):
                pbk = PB[c % 4]
                S.op("pe", fm_proj(1024 + c * 128, pbk), reads=[B_Wr, bh], writes=[pbk])
                S.op("act", (lambda c=c, pbk=pbk: lambda e: e.activation(out=sg[:, c, :], in_=pbk.ap[:, :],
                                                                          func=AF.Silu))(), reads=[pbk], writes=[B_sg])
            for ch in range(2):
                pbk = PB[4 + ch]
                S.op("pe", fm_proj(ch * 128, pbk), reads=[B_Wr, bh], writes=[pbk])
                S.op("act", (lambda ch=ch, pbk=pbk: lambda e: e.activation(out=u_sb[:, ch, :], in_=pbk.ap[:, :],
                                                                            func=AF.Copy))(), reads=[pbk], writes=[B_u])
            for ch in range(2):
                pbk = PB[ch]
                S.op("pe", fm_proj(512 + ch * 128, pbk), reads=[B_Wr, bh], writes=[pbk])
                S.op("dve", (lambda ch=ch, pbk=pbk: lambda e: e.tensor_tensor(out=cu[:, ch, 2:514], in0=pbk.ap[:, :],
                                                                               in1=u_sb[:, ch, :], op=ALU.mult))(),
                     reads=[pbk, B_u], writes=[B_cu])
            for ch in range(2):
                pbk = PB[2 + ch]
                S.op("pe", fm_proj(256 + ch * 128, pbk), reads=[B_Wr, bh], writes=[pbk])
                S.op("act", (lambda ch=ch, pbk=pbk: lambda e: e.activation(out=b_sb[:, ch, :], in_=pbk.ap[:, :],
                                                                            func=AF.Copy))(), reads=[pbk], writes=[B_b])
            for ch in range(2):
                pbk = PB[4 + ch]
                S.op("pe", fm_proj(768 + ch * 128, pbk), reads=[B_Wr, bh], writes=[pbk])
                for hh in range(2):
                    S.op("dve", (lambda ch=ch, pbk=pbk, hh=hh: lambda e: e.tensor_scalar(
                        out=qmh[2 * ch + hh], in0=pbk.ap[:, :], scalar1=cst[:, 2 + hh:3 + hh], scalar2=None,
                        op0=ALU.mult))(), reads=[pbk, B_cst], writes=[B_qm])
            for ch in range(2):
                def fconv0(e, ch=ch):
                    w0 = convw[:, ch * 3 + 0:ch * 3 + 1]
                    return e.tensor_scalar(out=t1[:, ch, :], in0=cu[:, ch, 0:512], scalar1=w0,
                                           scalar2=convb[:, ch:ch + 1], op0=ALU.mult, op1=ALU.add)

                def fconv1(e, ch=ch):
                    w1 = convw[:, ch * 3 + 1:ch * 3 + 2]
                    return e.scalar_tensor_tensor(out=t1[:, ch, :], in0=cu[:, ch, 1:513], scalar=w1, in1=t1[:, ch, :],
                                                  op0=ALU.mult, op1=ALU.add)

                def fconv2(e, ch=ch):
                    w2 = convw[:, ch * 3 + 2:ch * 3 + 3]
                    return e.scalar_tensor_tensor(out=t1[:, ch, :], in0=cu[:, ch, 2:514], scalar=w2, in1=t1[:, ch, :],
                                                  op0=ALU.mult, op1=ALU.add)
                S.op("dve", fconv0, reads=[B_cu, B_vecs], writes=[B_t1])
                S.op("dve", fconv1, reads=[B_cu, B_vecs, B_t1], writes=[B_t1])
                S.op("dve", fconv2, reads=[B_cu, B_vecs, B_t1], writes=[B_t1])
                S.op("dve", (lambda ch=ch: lambda e: e.tensor_tensor(out=yc[:, ch, :], in0=t1[:, ch, :],
                                                                      in1=b_sb[:, ch, :], op=ALU.mult))(),
                     reads=[B_t1, B_b], writes=[B_yc])
                S.op("dve", (lambda ch=ch: lambda e: e.tensor_copy(out=cu[:, ch, 0:2], in_=cu[:, ch, 512:514]))(),
                     reads=[B_cu], writes=[B_cu])
                S.op("act", (lambda ch=ch: lambda e: e.activation(out=ysq[:, 4 + ch, :], in_=yc[:, ch, :],
                                                                   func=AF.Square))(), reads=[B_yc], writes=[B_ysq])
                S.op("dve", (lambda ch=ch: lambda e: e.tensor_tensor(out=ysg[:, 4 + ch, :], in0=yc[:, ch, :],
                                                                      in1=sg[:, 4 + ch, :], op=ALU.mult))(),
                     reads=[B_yc, B_sg], writes=[B_ysg])
            marks.setdefault('p3c', len(S.ops))
            def fld(e, g=g):
                pid = e.partition_id()
                src = ycat_d[:, bass.ds(pid * TO + g * 512, 512)].rearrange("(c p) n -> p c n", p=128)
                return e.dma_start(out=ysb, in_=src)
            S.op("sync", fld, reads=[B_ycat], writes=[B_ysb], sem="ysbld", inc=16)
            S.op("act", lambda e: e.activation(out=ysq[:, 0:4, :], in_=ysb, func=AF.Square),
                 reads=[B_ysb], writes=[B_ysq])
            S.op("dve", lambda e: e.tensor_tensor(out=ysg[:, 0:4, :], in0=ysb, in1=sg[:, 0:4, :], op=ALU.mult),
                 reads=[B_ysb, B_sg], writes=[B_ysg])
            marks.setdefault('p3y', len(S.ops))

            for h in range(4):
                for mc in range(2):
                    kk_ = h * 2 + mc
                    pbk = PB[kk_ % 4]
                    S.op("pe", (lambda h=h, mc=mc, pbk=pbk: lambda e: e.matmul(
                        pbk.ap[:, :], lhsT=mkT[:, h // 2, mc * 128:(mc + 1) * 128], rhs=qmh[h], start=True, stop=True))(),
                        reads=[B_mkT, B_qm], writes=[pbk])
                    S.op("act", (lambda kk_=kk_, pbk=pbk: lambda e: e.activation(out=eT[:, kk_, :], in_=pbk.ap[:, :],
                                                                                  func=AF.Exp, scale=0.125))(),
                         reads=[pbk], writes=[B_eT[kk_]])
            marks.setdefault('p3e', len(S.ops))
            for c in range(2):
                py, pd = PB[4 + c], PB[6 + c]

                def fy(e, c=c, py=py):
                    n_ = 0
                    for h in (2 * c, 2 * c + 1):
                        for mc in range(2):
                            i_ = e.matmul(py.ap[:, :], lhsT=mvh[h][:, mc, :], rhs=eT[:, h * 2 + mc, :],
                                          start=(n_ == 0), stop=(n_ == 3))
                            n_ += 1
                    return i_
                S.op("pe", fy, reads=[B_mv] + B_eT[4 * c:4 * c + 4], writes=[py])

                def fd(e, c=c, pd=pd):
                    n_ = 0
                    for h in (2 * c, 2 * c + 1):
                        for mc in range(2):
                            i_ = e.matmul(pd.ap[:, :], lhsT=onesh[:, h % 2, :], rhs=eT[:, h * 2 + mc, :],
                                          start=(n_ == 0), stop=(n_ == 3))
                            n_ += 1
                    return i_
                S.op("pe", fd, reads=[B_cn] + B_eT[4 * c:4 * c + 4], writes=[pd])
                S.op("dve", (lambda pd=pd: lambda e: e.tensor_copy(out=rD, in_=pd.ap[:, :]))(), reads=[pd], writes=[B_rD])
                S.op("pool", lambda e: e.tensor_tensor(out=rD, in0=rD, in1=cneg[:, 1, :], op=ALU.pow),
                     reads=[B_cn], writes=[B_rD])
                S.op("dve", (lambda c=c, py=py: lambda e: e.tensor_tensor(out=t1[:, c, :], in0=py.ap[:, :], in1=rD,
                                                                           op=ALU.mult))(),
                     reads=[py, B_rD], writes=[B_t1])
                S.op("act", (lambda c=c: lambda e: e.activation(out=ysq[:, 6 + c, :], in_=t1[:, c, :],
                                                                 func=AF.Square))(), reads=[B_t1], writes=[B_ysq])
                S.op("dve", (lambda c=c: lambda e: e.tensor_tensor(out=ysg[:, 6 + c, :], in0=t1[:, c, :],
                                                                    in1=sg[:, 6 + c, :], op=ALU.mult))(),
                     reads=[B_t1, B_sg], writes=[B_ysg])
            marks.setdefault('p3t0', len(S.ops))
            for gi, (c0_, c1_, wdt) in enumerate(((0, 4, 512.0), (4, 6, 256.0), (6, 8, 256.0))):
                pbk = PB[gi]

                def fss(e, c0_=c0_, c1_=c1_, pbk=pbk):
                    for c in range(c0_, c1_):
                        i_ = e.matmul(pbk.ap[:, :], lhsT=onesh[:, 2, :], rhs=ysq[:, c, :], start=(c == c0_),
                                      stop=(c == c1_ - 1))
                    return i_
                S.op("pe", fss, reads=[B_ysq, B_cn], writes=[pbk])
                S.op("dve", (lambda pbk=pbk, wdt=wdt: lambda e: e.tensor_scalar(out=rsb, in0=pbk.ap[:, :], scalar1=1.0 / wdt,
                                                                               scalar2=EPS, op0=ALU.mult, op1=ALU.add))(),
                     reads=[pbk], writes=[B_rsb])
                S.op("pool", lambda e: e.tensor_tensor(out=rsb, in0=rsb, in1=cneg[:, 0, :], op=ALU.pow),
                     reads=[B_cn], writes=[B_rsb])
                S.op("dve", (lambda c0_=c0_, c1_=c1_: lambda e: e.tensor_tensor(
                    out=ysg[:, c0_:c1_, :], in0=ysg[:, c0_:c1_, :],
                    in1=bass.AP(tensor=rsb.tensor, offset=rsb.offset, ap=[list(rsb.ap[0]), [0, c1_ - c0_], list(rsb.ap[1])]),
                    op=ALU.mult))(), reads=[B_rsb], writes=[B_ysg])
            marks.setdefault('p3t1', len(S.ops))
            for r in range(4):
                i = g * 4 + r
                tsl = slice(r * 128, (r + 1) * 128)
                xr = xres[i % 2]
                S.op("sync", lambda e, i=i, xr=xr: e.dma_start(out=xr, in_=xo_d[128 + i * 128:128 + (i + 1) * 128, :]),
                     writes=[B_xres[i % 2]], sem=f"xres{i % 2}", inc=16)
                o = ob[i % 2]
                pa = [PB[4 + 2 * (i % 2)], PB[5 + 2 * (i % 2)]]

                def fop(e, tsl=tsl, pa=pa):
                    for half in range(2):
                        for c in range(8):
                            i_ = e.matmul(pa[half].ap[:, :], lhsT=ysg[:, c, tsl], rhs=Wo[:, c, half * 512:(half + 1) * 512],
                                          start=(c == 0), stop=(c == 7))
                    return i_
                S.op("pe", fop, reads=[B_ysg, B_Wo], writes=pa)

                def fcomb(e, o=o, xr=xr, pa=pa):
                    for half in range(2):
                        hs = slice(half * 512, (half + 1) * 512)
                        i_ = e.tensor_tensor(out=o[:, hs], in0=pa[half].ap[:, :], in1=xr[:, hs], op=ALU.add)
                    return i_
                S.op("dve", fcomb, reads=pa + [B_xres[i % 2]], writes=[B_ob[i % 2]])
                marks.setdefault('p3t2', len(S.ops))
                fs = st3[:, 32:33]
                fm = st3[:, 33:34]
                fr = st3[:, 34:35]
                S.op("act", lambda e, o=o: e.activation(out=junk3, in_=o, func=AF.Square, accum_out=fs),
                     reads=[B_ob[i % 2]], writes=[B_junk3, B_st3])
                S.seq("pool", [
                    lambda e: e.tensor_scalar(out=fm, in0=fs, scalar1=1.0 / D, scalar2=EPS, op0=ALU.mult, op1=ALU.add),
                    lambda e: e.tensor_tensor(out=fr, in0=fm, in1=cst[:, 0:1], op=ALU.pow)],
                    reads=[B_st3, B_cst], writes=[B_st3])
                S.op("dve", lambda e, o=o: e.scalar_tensor_tensor(out=o, in0=o, scalar=fr, in1=gfin, op0=ALU.mult,
                                                                  op1=ALU.mult),
                     reads=[B_st3, B_gfin, B_ob[i % 2]], writes=[B_ob[i % 2]])
                S.op("sync", lambda e, i=i, o=o: e.dma_start(out=out_d[i * 128:(i + 1) * 128, :], in_=o),
                     reads=[B_ob[i % 2]], sem=f"ost{i % 2}", inc=16)

        if stop is not None:
            S.ops = S.ops[:marks[stop]]
        final = {}
        for (en_, fn_, d_, sm_, inc_) in S.ops:
            final[sm_] = final.get(sm_, 0) + inc_

        names = sorted(final.keys())
        sems = {nm: es.enter_context(nc.semaphore(nm)) for nm in names}
        block = es.enter_context(nc.Block())
        engs = {"pe": block.tensor, "act": block.scalar, "dve": block.vector, "pool": block.gpsimd, "sync": block.sync}
        for name, deco in engs.items():
            def make(name):
                def body(e):
                    waited = {}
                    for (en, fn, d, sm, inc) in S.ops:
                        if en != name:
                            continue
                        for k_, v_ in d.items():
                            if name == "pe" and k_ == "pe":
                                continue
                            if waited.get(k_, 0) < v_:
                                e.wait_ge(sems[k_], v_)
                                waited[k_] = v_
                        inst = fn(e)
                        inst.then_inc(sems[sm], inc)
                    if name in ("pool", "sync"):
                        for k_, v_ in final.items():
                            if waited.get(k_, 0) < v_:
                                e.wait_ge(sems[k_], v_)
                return body
            deco(make(name))
    return nc


_CACHE = {}


def _host_inputs(x, mem, g_in, w_in, conv_w, conv_b, g_mem, w_mem_kv, g_sb_out, g_conv_out, g_mem_out, w_out, g_final):
    T = x.shape[1]
    TO = T // NCORES
    x2 = np.ascontiguousarray(x[0], dtype=np.float32)
    w = w_in[0]
    t128 = lambda v: np.ascontiguousarray(np.asarray(v, np.float32).reshape(-1, 128).T)
    vecs = np.zeros((128, 32), np.float32)
    vecs[:, 0:8] = t128(g_in[0])
    vecs[:, 8:16] = t128(g_mem[0])
    vecs[:, 16:24] = t128(np.concatenate([g_sb_out[0], g_conv_out[0], g_mem_out[0]]))
    cw = np.asarray(conv_w[0], np.float32)
    for ch in range(2):
        for i in range(3):
            vecs[:, 24 + ch * 3 + i] = cw[i, ch * 128:(ch + 1) * 128]
        vecs[:, 30 + ch] = np.asarray(conv_b[0], np.float32)[ch * 128:(ch + 1) * 128]
    gfin = np.ascontiguousarray(np.broadcast_to(np.asarray(g_final, np.float32)[None, :], (128, D)))
    q = np.arange(128)[:, None]
    kk = np.arange(128)[None, :]
    cb = np.zeros((128, 384), np.float32)
    cb[:, 0:128] = np.eye(128)
    cb[:, 128:256] = np.where(kk >= q, NEG, 0.0)
    cb[:, 256:384] = np.where(kk < q, 1.0, 0.0)
    cb = cb.astype(ml_dtypes.bfloat16)
    wrest = np.ascontiguousarray(w[:, 1536:3584], dtype=np.float32)
    wout = np.ascontiguousarray(w_out[0], dtype=np.float32)
    wmem = np.ascontiguousarray(w_mem_kv[0], dtype=np.float32)
    mem2 = np.ascontiguousarray(mem[0], dtype=np.float32)
    maps = []
    for c in range(NCORES):
        xo = np.zeros((128 + TO, D), np.float32)
        lo = c * TO - 128
        if lo >= 0:
            xo[:] = x2[lo:lo + 128 + TO]
        else:
            xo[128:] = x2[0:TO]
        wqkv = np.ascontiguousarray(
            np.concatenate([w[:, c * 64:(c + 1) * 64], w[:, 512 + c * 64:512 + (c + 1) * 64],
                            w[:, 1024 + c * 64:1024 + (c + 1) * 64]], axis=1), dtype=np.float32)
        maps.append({"x": x2, "xo": xo, "wqkv": wqkv, "wrest": wrest, "wout": wout, "wmem": wmem, "mem": mem2,
                     "vecs": vecs, "gfin": gfin, "cbf": cb})
    return T, maps


def kernel(x, mem, g_in, w_in, conv_w, conv_b, g_mem, w_mem_kv, g_sb_out, g_conv_out, g_mem_out, w_out, g_final):
    args = [np.asarray(a) for a in (x, mem, g_in, w_in, conv_w, conv_b, g_mem, w_mem_kv, g_sb_out, g_conv_out,
                                    g_mem_out, w_out, g_final)]
    T, maps = _host_inputs(*args)
    if T not in _CACHE:
        _CACHE[T] = build(T)
    nc = _CACHE[T]
    res = run_bass_kernel_spmd(nc, maps, core_ids=list(range(NCORES)))
    outs = [np.asarray(res.results[c]["out"], dtype=np.float32) for c in range(NCORES)]
    return np.concatenate(outs, axis=0)[None, :, :]
```

```python
import numpy as np
import ml_dtypes
import concourse.bass as bass
import concourse.mybir as mybir
from concourse.bass_utils import run_bass_kernel_spmd

F32 = mybir.dt.float32
BF16 = mybir.dt.bfloat16
AF = mybir.ActivationFunctionType
ALU = mybir.AluOpType
AX = mybir.AxisListType

D = 1024
NCORES = 8
EPS = 1e-6
NEG = -30000.0
CH = 512


class Buf:
    def __init__(self, ap=None):
        self.ap = ap
        self.rd = {}
        self.wr = {}


class Sched:
    def __init__(self):
        self.ops = []
        self.cnt = {}
        self.base = {}

    def op(self, eng, fn, reads=(), writes=(), sem=None, inc=1):
        sm = sem or eng
        d = dict(self.base)

        def add(td):
            for k, v in td.items():
                if d.get(k, 0) < v:
                    d[k] = v

        for b in reads:
            add(b.wr)
        for b in writes:
            add(b.wr)
            add(b.rd)
        self.cnt[sm] = self.cnt.get(sm, 0) + inc
        c = self.cnt[sm]
        for b in reads:
            b.rd[sm] = c
        for b in writes:
            b.wr[sm] = c
            b.rd = {}
        self.ops.append((eng, fn, d, sm, inc, c))
        return (sm, c)

    def seq(self, eng, fns, reads=(), writes=()):
        tb = Buf()
        for f in fns:
            t = self.op(eng, f, reads=list(reads), writes=list(writes) + [tb])
        return t

    def barrier(self):
        self.base = dict(self.cnt)

    def sem_names(self):
        return sorted(self.cnt.keys())


def bc_last(ap, n):
    return bass.AP(tensor=ap.tensor, offset=ap.offset, ap=[list(a) for a in ap.ap] + [[0, n]])


def build(T, stop=None):
    marks = {}
    NT = T // 128
    TO = T // NCORES
    NTO = TO // 128
    NG1 = T // 512
    NG3 = TO // 512
    assert T % (512 * NCORES) == 0

    nc = bass.Bass("TRN2", target_bir_lowering=False)
    dt = nc.dram_tensor
    x_d = dt("x", [T, D], F32, kind="ExternalInput").ap()
    xo_d = dt("xo", [128 + TO, D], F32, kind="ExternalInput").ap()
    wqkv_d = dt("wqkv", [D, 192], F32, kind="ExternalInput").ap()
    wrest_d = dt("wrest", [D, 2048], F32, kind="ExternalInput").ap()
    wout_d = dt("wout", [D, D], F32, kind="ExternalInput").ap()
    wmem_d = dt("wmem", [D, 512], F32, kind="ExternalInput").ap()
    mem_d = dt("mem", [256, D], F32, kind="ExternalInput").ap()
    vecs_d = dt("vecs", [128, 32], F32, kind="ExternalInput").ap()
    gfin_d = dt("gfin", [128, D], F32, kind="ExternalInput").ap()
    cbf_d = dt("cbf", [128, 384], BF16, kind="ExternalInput").ap()
    out_d = dt("out", [TO, D], F32, kind="ExternalOutput").ap()
    yloc_d = dt("yloc", [64, T], BF16, kind="Internal").ap()
    ycat_d = dt("ycat", [512, T], BF16, kind="Internal").ap()

    S = Sched()
    ARENA_W = 49 * 1024

    import contextlib
    with contextlib.ExitStack() as es:
        arena = es.enter_context(nc.sbuf_tensor("arena", [128, ARENA_W], F32))
        cbf = es.enter_context(nc.sbuf_tensor("cbf_sb", [128, 384], BF16))
        vecs = es.enter_context(nc.sbuf_tensor("vecs_sb", [128, 32], F32))
        zer = es.enter_context(nc.sbuf_tensor("zer", [128, 512], F32))
        cst = es.enter_context(nc.sbuf_tensor("cst", [128, 16], F32))
        onesb = es.enter_context(nc.sbuf_tensor("onesb", [128, 2], BF16))
        stat = es.enter_context(nc.sbuf_tensor("stat", [128, 64], F32))
        banks = [es.enter_context(nc.psum_tensor(f"pb{i}", [128, 512], F32)) for i in range(8)]
        PB = [Buf(b) for b in banks]

        ident = cbf[:, 0:128]
        mneg = cbf[:, 128:256]
        mask01 = cbf[:, 256:384]
        B_cbf, B_vecs, B_zer, B_cst, B_ones = Buf(), Buf(), Buf(), Buf(), Buf()
        gin = vecs[:, 0:8]
        gmem = vecs[:, 8:16]
        gcat = vecs[:, 16:24]
        convw = vecs[:, 24:30]
        convb = vecs[:, 30:32]

        state = {"off": 0}

        def alloc(shape_free, dtype):
            nb = int(np.prod(shape_free)) * (2 if dtype == BF16 else 4)
            nb4 = (nb + 3) // 4
            o = state["off"]
            assert o + nb4 <= ARENA_W, f"arena overflow {o + nb4} > {ARENA_W}"
            state["off"] = o + nb4
            ap = arena[:, o:o + nb4]
            if dtype == BF16:
                ap = ap.bitcast(BF16)
                n = int(np.prod(shape_free))
                ap = ap[:, 0:n]
            if len(shape_free) == 2:
                ap = ap.rearrange("p (a b) -> p a b", a=shape_free[0])
            elif len(shape_free) == 3:
                ap = ap.rearrange("p (a b c) -> p a b c", a=shape_free[0], b=shape_free[1])
            return ap

        S.op("sync", lambda e: e.dma_start(out=cbf[:], in_=cbf_d[:, :]), writes=[B_cbf], sem="c0", inc=16)
        S.op("sync", lambda e: e.dma_start(out=vecs[:], in_=vecs_d[:, :]), writes=[B_vecs], sem="c1", inc=16)
        S.op("dve", lambda e: e.memset(zer[:], 0.0), writes=[B_zer])

        def _cst(e):
            e.memset(cst[:, 0:1], -0.5)
            e.memset(cst[:, 1:2], EPS)
            e.memset(cst[:, 4:8], -1.0)
            e.memset(cst[:, 8:14], -0.5)
            e.memset(cst[0:64, 2:3], 1.0)
            e.memset(cst[64:128, 2:3], 0.0)
            e.memset(cst[0:64, 3:4], 0.0)
            e.memset(cst[64:128, 3:4], 1.0)
            return e.memset(onesb[:], 1.0)
        S.op("pool", _cst, writes=[B_cst, B_ones])

        def make_norm_pipe():
            xt = [alloc([D], F32) for _ in range(3)]
            junk = alloc([D], BF16)
            xn = [alloc([D], BF16) for _ in range(2)]
            st = {"k": 0}
            bxt = [Buf() for _ in range(3)]
            bjunk = Buf()
            bxn = [Buf() for _ in range(2)]
            bst = [Buf() for _ in range(4)]

            def run(src_rows, dst, dst_buf, evac_eng, keep_x=None):
                k = st["k"]
                st["k"] += 1
                a, b2, s4 = k % 3, k % 2, k % 4
                xta, xnb = xt[a], xn[b2]
                ss = stat[:, 4 * s4:4 * s4 + 1]
                ms = stat[:, 4 * s4 + 1:4 * s4 + 2]
                rs = stat[:, 4 * s4 + 2:4 * s4 + 3]
                S.op("sync", lambda e: e.dma_start(out=xta, in_=src_rows), writes=[bxt[a]], sem=f"xld{a}", inc=16)
                S.op("act", lambda e: e.activation(out=junk, in_=xta, func=AF.Square, accum_out=ss),
                     reads=[bxt[a]], writes=[bjunk, bst[s4]])

                S.seq("act", [
                    lambda e: e.activation(out=ms, in_=ss, func=AF.Ln, scale=1.0 / D, bias=cst[:, 1:2]),
                    lambda e: e.activation(out=rs, in_=ms, func=AF.Exp, scale=-0.5)],
                    reads=[B_cst], writes=[bst[s4]])
                S.op("dve", lambda e: e.tensor_scalar(out=xnb, in0=xta, scalar1=rs, scalar2=None, op0=ALU.mult),
                     reads=[bxt[a], bst[s4]], writes=[bxn[b2]])
                pbk = PB[6 + b2]
                pv = pbk.ap[:, :].bitcast(BF16).rearrange("p (j t) -> p j t", j=8)

                def _tp(e):
                    for j in range(8):
                        i = e.transpose(out=pv[:, j, :], in_=xnb[:, j * 128:(j + 1) * 128], identity=ident)
                    return i
                S.op("pe", _tp, reads=[bxn[b2], B_cbf], writes=[pbk])
                if evac_eng == "act":
                    S.op("act", lambda e: e.activation(out=dst, in_=pv, func=AF.Copy), reads=[pbk], writes=[dst_buf])
                else:
                    S.op("dve", lambda e: e.tensor_copy(out=dst, in_=pv), reads=[pbk], writes=[dst_buf])
                return xta, bxt[a]
            return run

        mark0 = state["off"]
        Qt = alloc([T], BF16)
        Kt = alloc([T], BF16)
        Vt = alloc([T + 2], BF16)
        Wtok = alloc([NT, 128], BF16)
        B_Qt, B_Kt, B_Vt, B_W = Buf(), Buf(), Buf(), Buf()
        mark_work = state["off"]

        wst = alloc([8, 192], F32)
        wq = alloc([8, 128], BF16)
        wk = alloc([8, 128], BF16)
        wv = alloc([8, 128], BF16)
        B_wst, B_wqkv = Buf(), Buf()
        S.op("sync", lambda e: e.dma_start(out=wst, in_=wqkv_d.rearrange("(j p) c -> p j c", p=128)),
             writes=[B_wst], sem="c2", inc=16)

        def _wzero(e):
            e.memset(wq, 0.0)
            e.memset(wk, 0.0)
            e.memset(wv, 0.0)
            return e.memset(Vt, 0.0)

        def _wprep(e):
            g64 = bc_last(gin, 64)
            e.tensor_tensor(out=wq[:, :, 0:64], in0=wst[:, :, 0:64], in1=g64, op=ALU.mult)
            e.tensor_tensor(out=wk[:, :, 0:64], in0=wst[:, :, 64:128], in1=g64, op=ALU.mult)
            return e.tensor_tensor(out=wv[:, :, 0:64], in0=wst[:, :, 128:192], in1=g64, op=ALU.mult)
        S.seq("dve", [_wzero, _wprep], reads=[B_wst, B_vecs], writes=[B_wqkv, B_Vt])

        norm1 = make_norm_pipe()
        hTg = [alloc([8, 512], BF16) for _ in range(2)]
        B_hTg = [Buf() for _ in range(2)]
        wTg = [alloc([512], BF16) for _ in range(2)]
        B_wTg = [Buf() for _ in range(2)]

        for g in range(NG1):
            hb = hTg[g % 2]
            bh = B_hTg[g % 2]
            for r in range(4):
                i = g * 4 + r
                norm1(x_d[i * 128:(i + 1) * 128, :], hb[:, :, r * 128:(r + 1) * 128], bh,
                      "act" if (i % 2 == 0) else "dve")
            c0 = g * 512

            def proj(w_sb, pbk, hb=hb):
                def f(e):
                    for j in range(8):
                        i_ = e.matmul(pbk.ap[:, :], lhsT=w_sb[:, j, :], rhs=hb[:, j, :], start=(j == 0), stop=(j == 7))
                    return i_
                return f
            S.op("pe", proj(wq, PB[0]), reads=[bh, B_wqkv], writes=[PB[0]])
            S.op("act", lambda e, c0=c0: e.activation(out=Qt[:, c0:c0 + 512], in_=PB[0].ap[:, :], func=AF.Copy),
                 reads=[PB[0]], writes=[B_Qt])
            S.op("pe", proj(wk, PB[1]), reads=[bh, B_wqkv], writes=[PB[1]])
            S.op("dve", lambda e, c0=c0: e.tensor_copy(out=Kt[:, c0:c0 + 512], in_=PB[1].ap[:, :]),
                 reads=[PB[1]], writes=[B_Kt])
            S.op("pe", proj(wv, PB[2]), reads=[bh, B_wqkv], writes=[PB[2]])
            S.op("act", lambda e, c0=c0: e.activation(out=Vt[:, 1 + c0:1 + c0 + 512], in_=PB[2].ap[:, :], func=AF.Copy),
                 reads=[PB[2]], writes=[B_Vt])
            wt = wTg[g % 2]
            bw = B_wTg[g % 2]
            S.op("dve", lambda e, c0=c0, wt=wt: e.tensor_tensor(out=wt, in0=Vt[:, c0:c0 + 512],
                                                                in1=Vt[:, c0 + 1:c0 + 513], op=ALU.subtract),
                 reads=[B_Vt], writes=[bw])
            pw = PB[3].ap[:, 0:256].bitcast(BF16).rearrange("p (a b) -> p a b", a=4)

            def _wtp(e, wt=wt, pw=pw):
                for r in range(4):
                    i_ = e.transpose(out=pw[:, r, :], in_=wt[:, r * 128:(r + 1) * 128], identity=ident)
                return i_
            S.op("pe", _wtp, reads=[bw, B_cbf], writes=[PB[3]])
            S.op("dve", lambda e, g=g, pw=pw: e.tensor_copy(out=Wtok[:, g * 4:(g + 1) * 4, :], in_=pw),
                 reads=[PB[3]], writes=[B_W])

        marks['p1'] = len(S.ops)
        S.barrier()
        state["off"] = mark_work
        NZ, NNB, NPIN, NPT, NPTS = 3, 3, 4, 2, 4
        nb_sb = [alloc([CH], F32) for _ in range(NNB)]
        pin = [alloc([CH], BF16) for _ in range(NPIN)]
        pts = [alloc([CH], BF16) for _ in range(NPTS)]
        ybuf = [alloc([128], BF16) for _ in range(2)]
        B_nb = [Buf() for _ in range(NNB)]
        B_pin = [Buf() for _ in range(NPIN)]
        B_pts = [Buf() for _ in range(NPTS)]
        B_yb = [Buf() for _ in range(2)]
        ZB = [PB[0], PB[1], PB[2]]
        PTB = [PB[3], PB[4]]
        ACC = [PB[5], PB[6]]

        def tile_chunks(i):
            lst = []
            e_ = (i + 1) * 128
            j = 0
            while e_ - CH * j > 0:
                b = e_ - CH * j
                a = max(0, b - CH)
                lst.append([i, j, a, b, j == 0, a == 0])
                j += 1
            return lst
        chunks = []
        for i0_ in range(0, NT, 2):
            la, lb = tile_chunks(i0_), tile_chunks(i0_ + 1)
            pa_, pb_ = None, None
            for q_ in range(max(len(la), len(lb))):
                if q_ < len(la):
                    chunks.append(la[q_] + [pa_])
                    pa_ = len(chunks) - 1
                if q_ < len(lb):
                    chunks.append(lb[q_] + [pb_])
                    pb_ = len(chunks) - 1
        NCk = len(chunks)

        def qk(n):
            i, j, a, b, first, last, prevn = chunks[n]
            sz = b - a
            zb = ZB[n % NZ]
            t0 = i * 128
            qT = Qt[:, t0:t0 + 128]

            def f(e):
                if first:
                    if sz > 128:
                        e.matmul(zb.ap[:, 0:sz - 128], lhsT=qT, rhs=Kt[:, a:b - 128], start=True, stop=True)
                    e.matmul(zb.ap[:, sz - 128:sz], lhsT=qT, rhs=Kt[:, b - 128:b], start=True, stop=False)
                    return e.matmul(zb.ap[:, sz - 128:sz], lhsT=ident, rhs=mneg, start=False, stop=True)
                return e.matmul(zb.ap[:, 0:sz], lhsT=qT, rhs=Kt[:, a:b], start=True, stop=True)
            S.op("pe", f, reads=[B_Qt, B_Kt, B_cbf], writes=[zb])

        def sig(n):
            i, j, a, b, first, last, prevn = chunks[n]
            sz = b - a
            zb = ZB[n % NZ]
            o = nb_sb[n % NNB]
            S.op("act", lambda e: e.activation(out=o[:, 0:sz], in_=zb.ap[:, 0:sz], func=AF.Sigmoid, scale=-0.125),
                 reads=[zb], writes=[B_nb[n % NNB]])

        def scan(n):
            i, j, a, b, first, last, prevn = chunks[n]
            sz = b - a
            src = nb_sb[n % NNB]
            dst = pin[n % NPIN]
            prev = pin[(prevn if prevn is not None else 0) % NPIN]

            def f(e):
                init = 1.0 if first else prev[:, 0:1]
                i_ = e.tensor_tensor_scan(out=dst[:, 0:sz][:, ::-1], data0=src[:, 0:sz][:, ::-1], data1=zer[:, 0:sz],
                                          initial=init, op0=ALU.mult, op1=ALU.add)
                return i_
            rd = [B_nb[n % NNB], B_zer] + ([] if first else [B_pin[prevn % NPIN]])
            S.op("dve", f, reads=rd, writes=[B_pin[n % NPIN]])
            if first:
                S.op("dve", lambda e: e.tensor_tensor(out=dst[:, sz - 128:sz], in0=dst[:, sz - 128:sz], in1=mask01,
                                                      op=ALU.mult), reads=[B_cbf], writes=[B_pin[n % NPIN]])

        def tp(n):
            i, j, a, b, first, last, prevn = chunks[n]
            sz = b - a
            src = pin[n % NPIN]
            pb = PTB[n % NPT]
            pv = pb.ap[:, :].bitcast(BF16)

            def f(e):
                for blk in range(sz // 128):
                    i_ = e.transpose(out=pv[:, blk * 128:(blk + 1) * 128], in_=src[:, blk * 128:(blk + 1) * 128],
                                     identity=ident)
                return i_
            S.op("pe", f, reads=[B_pin[n % NPIN], B_cbf], writes=[pb])

        def evac(n):
            i, j, a, b, first, last, prevn = chunks[n]
            sz = b - a
            pb = PTB[n % NPT]
            pv = pb.ap[:, :].bitcast(BF16)
            o = pts[n % NPTS]
            S.op("act", lambda e: e.activation(out=o[:, 0:sz], in_=pv[:, 0:sz], func=AF.Copy),
                 reads=[pb], writes=[B_pts[n % NPTS]])

        def av(n):
            i, j, a, b, first, last, prevn = chunks[n]
            sz = b - a
            src = pts[n % NPTS]
            acc = ACC[i % 2]
            nblk = sz // 128

            def f(e):
                for blk in range(nblk):
                    kt = a // 128 + blk
                    i_ = e.matmul(acc.ap[:, 0:128], lhsT=Wtok[:, kt, :], rhs=src[:, blk * 128:(blk + 1) * 128],
                                  start=(first and blk == 0), stop=(last and blk == nblk - 1))
                return i_
            S.op("pe", f, reads=[B_pts[n % NPTS], B_W], writes=[acc])
            if last:
                t0 = i * 128
                yb = ybuf[i % 2]
                S.op("dve", lambda e: e.tensor_tensor(out=yb[0:64, :], in0=acc.ap[0:64, 0:128], in1=Vt[0:64, t0:t0 + 128],
                                                      op=ALU.add), reads=[acc, B_Vt], writes=[B_yb[i % 2]])
                S.op("pool", lambda e: e.dma_start(out=yloc_d[:, t0:t0 + 128], in_=yb[0:64, :]),
                     reads=[B_yb[i % 2]], sem=f"yst{i % 2}", inc=16, writes=[])

        for n in range(-2, NCk + 2):
            if 0 <= n + 2 < NCk:
                qk(n + 2)
            if 0 <= n + 1 < NCk:
                sig(n + 1)
            if 0 <= n < NCk:
                scan(n)
                tp(n)
            if 0 <= n - 1 < NCk:
                evac(n - 1)
            if 0 <= n - 2 < NCk:
                av(n - 2)

        marks['p2'] = len(S.ops)
        S.barrier()
        B_ycat = Buf()
        S.op("pool", lambda e: e.collective_compute("AllGather", ALU.bypass, replica_groups=[list(range(NCORES))],
                                                    ins=[yloc_d[:, :]], outs=[ycat_d[:, :]]),
             writes=[B_ycat], sem="cc", inc=1)
        S.barrier()

        marks['ag'] = len(S.ops)
        state["off"] = mark0
        Wr = alloc([8, 2048], BF16)
        Wo = alloc([8, D], BF16)
        Wm = alloc([8, 512], BF16)
        B_Wr, B_Wo, B_Wm = Buf(), Buf(), Buf()
        stg = [alloc([1024], F32) for _ in range(2)]
        B_stg = [Buf() for _ in range(2)]
        gfin = alloc([D], F32)
        B_gfin = Buf()
        S.op("sync", lambda e: e.dma_start(out=gfin, in_=gfin_d[:, :]), writes=[B_gfin], sem="c3", inc=16)
        k = 0
        for (src_d, ncol, dstw, gv, bdst) in ((wrest_d, 2048, Wr, gin, B_Wr), (wout_d, D, Wo, gcat, B_Wo),
                                              (wmem_d, 512, Wm, gmem, B_Wm)):
            for j in range(8):
                for c0_ in range(0, ncol, 1024):
                    w_ = min(1024, ncol - c0_)
                    sg_, bs_ = stg[k % 2], B_stg[k % 2]
                    S.op("sync", (lambda sg_=sg_, j=j, src_d=src_d, c0_=c0_, w_=w_:
                                  lambda e: e.dma_start(out=sg_[:, 0:w_], in_=src_d[j * 128:(j + 1) * 128, c0_:c0_ + w_]))(),
                         writes=[bs_], sem=f"stg{k % 2}", inc=16)
                    if k % 2 == 0:
                        S.op("dve", (lambda sg_=sg_, j=j, dstw=dstw, gv=gv, c0_=c0_, w_=w_:
                                     lambda e: e.tensor_scalar(out=dstw[:, j, c0_:c0_ + w_], in0=sg_[:, 0:w_],
                                                               scalar1=gv[:, j:j + 1], scalar2=None, op0=ALU.mult))(),
                             reads=[bs_, B_vecs], writes=[bdst])
                    else:
                        S.op("act", (lambda sg_=sg_, j=j, dstw=dstw, gv=gv, c0_=c0_, w_=w_:
                                     lambda e: e.activation(out=dstw[:, j, c0_:c0_ + w_], in_=sg_[:, 0:w_],
                                                            func=AF.Copy, scale=gv[:, j:j + 1]))(),
                             reads=[bs_, B_vecs], writes=[bdst])
                    k += 1

        marks['p3w'] = len(S.ops)
        norm3 = make_norm_pipe()
        memT = alloc([8, 256], BF16)
        B_memT = Buf()
        for r in range(2):
            norm3(mem_d[r * 128:(r + 1) * 128, :], memT[:, :, r * 128:(r + 1) * 128], B_memT, "dve")
        mkT = alloc([2, 256], BF16)
        mvh = [alloc([2, 128], BF16) for _ in range(4)]
        B_mkT, B_mv = Buf(), Buf()

        def _mvz(e):
            for h in range(4):
                i_ = e.memset(mvh[h], 0.0)
            return i_
        S.op("dve", _mvz, writes=[B_mv])
        for c in range(2):
            def f(e, c=c):
                for j in range(8):
                    i_ = e.matmul(PB[0].ap[:, 0:256], lhsT=Wm[:, j, c * 128:(c + 1) * 128], rhs=memT[:, j, :],
                                  start=(j == 0), stop=(j == 7))
                return i_
            S.op("pe", f, reads=[B_Wm, B_memT], writes=[PB[0]])
            S.op("dve", (lambda c=c: lambda e: e.tensor_copy(out=mkT[:, c, :], in_=PB[0].ap[:, 0:256]))(),
                 reads=[PB[0]], writes=[B_mkT])
        for mc in range(2):
            def f(e, mc=mc):
                for j in range(8):
                    i_ = e.matmul(PB[1].ap[:, 0:256], lhsT=memT[:, j, mc * 128:(mc + 1) * 128], rhs=Wm[:, j, 256:512],
                                  start=(j == 0), stop=(j == 7))
                return i_
            S.op("pe", f, reads=[B_Wm, B_memT], writes=[PB[1]])
            def fmv(e, mc=mc):
                for h in range(4):
                    o0 = (h % 2) * 64
                    i_ = e.tensor_copy(out=mvh[h][:, mc, o0:o0 + 64], in_=PB[1].ap[:, h * 64:(h + 1) * 64])
                return i_
            S.op("dve", fmv, reads=[PB[1]], writes=[B_mv])

        marks['p3m'] = len(S.ops)
        hTo = [alloc([8, 512], BF16) for _ in range(2)]
        B_hTo = [Buf() for _ in range(2)]
        cu = alloc([2, 514], F32)
        B_cu = Buf()
        u_sb = alloc([2, 512], F32)
        b_sb = alloc([2, 512], F32)
        B_u, B_b = Buf(), Buf()
        sg = alloc([8, 512], BF16)
        B_sg = Buf()
        qmh = [alloc([512], BF16) for _ in range(4)]
        B_qm = Buf()
        t1 = alloc([2, 512], F32)
        B_t1 = Buf()
        yc = alloc([2, 512], F32)
        B_yc = Buf()
        ysb = alloc([4, 512], BF16)
        B_ysb = Buf()
        ysg = alloc([8, 512], BF16)
        B_ysg = Buf()
        ysq = alloc([8, 512], BF16)
        B_ysq = Buf()
        eT = alloc([8, 512], BF16)
        B_eT = [Buf() for _ in range(8)]
        rsb = alloc([512], F32)
        B_rsb = Buf()
        rD = alloc([512], F32)
        B_rD = Buf()
        cneg = alloc([2, 512], F32)
        onesh = alloc([3, 128], BF16)
        B_cn = Buf()
        xres = [alloc([D], F32) for _ in range(2)]
        B_xres = [Buf() for _ in range(2)]
        ob = [alloc([D], F32) for _ in range(2)]
        B_ob = [Buf() for _ in range(2)]
        junk3 = alloc([D], BF16)
        B_junk3 = Buf()
        st3 = alloc([40], F32)
        B_st3 = Buf()

        def _cn(e):
            e.memset(cneg[:, 0, :], -0.5)
            e.memset(cneg[:, 1, :], -1.0)
            e.memset(onesh[:, 0, 0:64], 1.0)
            e.memset(onesh[:, 0, 64:128], 0.0)
            e.memset(onesh[:, 1, 0:64], 0.0)
            e.memset(onesh[:, 1, 64:128], 1.0)
            return e.memset(onesh[:, 2, :], 1.0)
        S.op("dve", _cn, writes=[B_cn])

        norm3(xo_d[0:128, :], hTo[0][:, :, 0:128], B_hTo[0], "dve")
        hb0 = hTo[0]
        for ch in range(2):
            def fu(e, ch=ch):
                for j in range(8):
                    i_ = e.matmul(PB[0].ap[:, 0:128], lhsT=Wr[:, j, ch * 128:(ch + 1) * 128], rhs=hb0[:, j, 0:128],
                                  start=(j == 0), stop=(j == 7))
                return i_
            S.op("pe", fu, reads=[B_Wr, B_hTo[0]], writes=[PB[0]])
            S.op("act", (lambda ch=ch: lambda e: e.activation(out=u_sb[:, ch, 0:128], in_=PB[0].ap[:, 0:128],
                                                               func=AF.Copy))(), reads=[PB[0]], writes=[B_u])

            def fc(e, ch=ch):
                for j in range(8):
                    i_ = e.matmul(PB[1].ap[:, 0:128], lhsT=Wr[:, j, 512 + ch * 128:512 + (ch + 1) * 128],
                                  rhs=hb0[:, j, 0:128], start=(j == 0), stop=(j == 7))
                return i_
            S.op("pe", fc, reads=[B_Wr, B_hTo[0]], writes=[PB[1]])
            S.op("dve", (lambda ch=ch: lambda e: e.tensor_tensor(out=cu[:, ch, 0:2], in0=PB[1].ap[:, 126:128],
                                                                  in1=u_sb[:, ch, 126:128], op=ALU.mult))(),
                 reads=[PB[1], B_u], writes=[B_cu])

        marks['p3h'] = len(S.ops)
        for g in range(NG3):
            hb = hTo[(g + 1) % 2]
            bh = B_hTo[(g + 1) % 2]
            for r in range(4):
                i = g * 4 + r
                norm3(xo_d[128 + i * 128:128 + (i + 1) * 128, :], hb[:, :, r * 128:(r + 1) * 128], bh,
                      "act" if (i % 2 == 0) else "dve")

            def fm_proj(col0, pbk, n=512, hb=hb):
                def f(e):
                    for j in range(8):
                        i_ = e.matmul(pbk.ap[:, 0:n], lhsT=Wr[:, j, col0:col0 + 128], rhs=hb[:, j, 0:n],
                                      start=(j == 0), stop=(j == 7))
                    return i_
                return f
            for c in range(8):
                pbk = PB[c % 4]
                S.op("pe", fm_proj(1024 + c * 128, pbk), reads=[B_Wr, bh], writes=[pbk])
                S.op("act", (lambda c=c, pbk=pbk: lambda e: e.activation(out=sg[:, c, :], in_=pbk.ap[:, :],
                                                                          func=AF.Silu))(), reads=[pbk], writes=[B_sg])
            for ch in range(2):
                pbk = PB[4 + ch]
                S.op("pe", fm_proj(ch * 128, pbk), reads=[B_Wr, bh], writes=[pbk])
                S.op("act", (lambda ch=ch, pbk=pbk: lambda e: e.activation(out=u_sb[:, ch, :], in_=pbk.ap[:, :],
                                                                            func=AF.Copy))(), reads=[pbk], writes=[B_u])
            for ch in range(2):
                pbk = PB[ch]
                S.op("pe", fm_proj(512 + ch * 128, pbk), reads=[B_Wr, bh], writes=[pbk])
                S.op("dve", (lambda ch=ch, pbk=pbk: lambda e: e.tensor_tensor(out=cu[:, ch, 2:514], in0=pbk.ap[:, :],
                                                                               in1=u_sb[:, ch, :], op=ALU.mult))(),
                     reads=[pbk, B_u], writes=[B_cu])
            for ch in range(2):
                pbk = PB[2 + ch]
                S.op("pe", fm_proj(256 + ch * 128, pbk), reads=[B_Wr, bh], writes=[pbk])
                S.op("act", (lambda ch=ch, pbk=pbk: lambda e: e.activation(out=b_sb[:, ch, :], in_=pbk.ap[:, :],
                                                                            func=AF.Copy))(), reads=[pbk], writes=[B_b])
            for ch in range(2):
                pbk = PB[4 + ch]
                S.op("pe", fm_proj(768 + ch * 128, pbk), reads=[B_Wr, bh], writes=[pbk])
                for hh in range(2):
                    S.op("dve", (lambda ch=ch, pbk=pbk, hh=hh: lambda e: e.tensor_scalar(
                        out=qmh[2 * ch + hh], in0=pbk.ap[:, :], scalar1=cst[:, 2 + hh:3 + hh], scalar2=None,
                        op0=ALU.mult))(), reads=[pbk, B_cst], writes=[B_qm])
            for ch in range(2):
                def fconv0(e, ch=ch):
                    w0 = convw[:, ch * 3 + 0:ch * 3 + 1]
                    return e.tensor_scalar(out=t1[:, ch, :], in0=cu[:, ch, 0:512], scalar1=w0,
                                           scalar2=convb[:, ch:ch + 1], op0=ALU.mult, op1=ALU.add)

                def fconv1(e, ch=ch):
                    w1 = convw[:, ch * 3 + 1:ch * 3 + 2]
                    return e.scalar_tensor_tensor(out=t1[:, ch, :], in0=cu[:, ch, 1:513], scalar=w1, in1=t1[:, ch, :],
                                                  op0=ALU.mult, op1=ALU.add)

                def fconv2(e, ch=ch):
                    w2 = convw[:, ch * 3 + 2:ch * 3 + 3]
                    return e.scalar_tensor_tensor(out=t1[:, ch, :], in0=cu[:, ch, 2:514], scalar=w2, in1=t1[:, ch, :],
                                                  op0=ALU.mult, op1=ALU.add)
                S.op("dve", fconv0, reads=[B_cu, B_vecs], writes=[B_t1])
                S.op("dve", fconv1, reads=[B_cu, B_vecs, B_t1], writes=[B_t1])
                S.op("dve", fconv2, reads=[B_cu, B_vecs, B_t1], writes=[B_t1])
                S.op("dve", (lambda ch=ch: lambda e: e.tensor_tensor(out=yc[:, ch, :], in0=t1[:, ch, :],
                                                                      in1=b_sb[:, ch, :], op=ALU.mult))(),
                     reads=[B_t1, B_b], writes=[B_yc])
                S.op("dve", (lambda ch=ch: lambda e: e.tensor_copy(out=cu[:, ch, 0:2], in_=cu[:, ch, 512:514]))(),
                     reads=[B_cu], writes=[B_cu])
                S.op("act", (lambda ch=ch: lambda e: e.activation(out=ysq[:, 4 + ch, :], in_=yc[:, ch, :],
                                                                   func=AF.Square))(), reads=[B_yc], writes=[B_ysq])
                S.op("dve", (lambda ch=ch: lambda e: e.tensor_tensor(out=ysg[:, 4 + ch, :], in0=yc[:, ch, :],
                                                                      in1=sg[:, 4 + ch, :], op=ALU.mult))(),
                     reads=[B_yc, B_sg], writes=[B_ysg])
            marks.setdefault('p3c', len(S.ops))
            def fld(e, g=g):
                pid = e.partition_id()
                src = ycat_d[:, bass.ds(pid * TO + g * 512, 512)].rearrange("(c p) n -> p c n", p=128)
                return e.dma_start(out=ysb, in_=src)
            S.op("sync", fld, reads=[B_ycat], writes=[B_ysb], sem="ysbld", inc=16)
            S.op("act", lambda e: e.activation(out=ysq[:, 0:4, :], in_=ysb, func=AF.Square),
                 reads=[B_ysb], writes=[B_ysq])
            S.op("dve", lambda e: e.tensor_tensor(out=ysg[:, 0:4, :], in0=ysb, in1=sg[:, 0:4, :], op=ALU.mult),
                 reads=[B_ysb, B_sg], writes=[B_ysg])
            marks.setdefault('p3y', len(S.ops))

            for h in range(4):
                for mc in range(2):
                    kk_ = h * 2 + mc
                    pbk = PB[kk_ % 4]
                    S.op("pe", (lambda h=h, mc=mc, pbk=pbk: lambda e: e.matmul(
                        pbk.ap[:, :], lhsT=mkT[:, h // 2, mc * 128:(mc + 1) * 128], rhs=qmh[h], start=True, stop=True))(),
                        reads=[B_mkT, B_qm], writes=[pbk])
                    S.op("act", (lambda kk_=kk_, pbk=pbk: lambda e: e.activation(out=eT[:, kk_, :], in_=pbk.ap[:, :],
                                                                                  func=AF.Exp, scale=0.125))(),
                         reads=[pbk], writes=[B_eT[kk_]])
            marks.setdefault('p3e', len(S.ops))
            for c in range(2):
                py, pd = PB[4 + c], PB[6 + c]

                def fy(e, c=c, py=py):
                    n_ = 0
                    for h in (2 * c, 2 * c + 1):
                        for mc in range(2):
                            i_ = e.matmul(py.ap[:, :], lhsT=mvh[h][:, mc, :], rhs=eT[:, h * 2 + mc, :],
                                          start=(n_ == 0), stop=(n_ == 3))
                            n_ += 1
                    return i_
                S.op("pe", fy, reads=[B_mv] + B_eT[4 * c:4 * c + 4], writes=[py])

                def fd(e, c=c, pd=pd):
                    n_ = 0
                    for h in (2 * c, 2 * c + 1):
                        for mc in range(2):
                            i_ = e.matmul(pd.ap[:, :], lhsT=onesh[:, h % 2, :], rhs=eT[:, h * 2 + mc, :],
                                          start=(n_ == 0), stop=(n_ == 3))
                            n_ += 1
                    return i_
                S.op("pe", fd, reads=[B_cn] + B_eT[4 * c:4 * c + 4], writes=[pd])
                S.op("act", (lambda pd=pd: lambda e: e.activation(out=rD, in_=pd.ap[:, :], func=AF.Ln))(),
                     reads=[pd], writes=[B_rD])
                S.op("act", lambda e: e.activation(out=rD, in_=rD, func=AF.Exp, scale=-1.0), reads=[], writes=[B_rD])
                S.op("dve", (lambda c=c, py=py: lambda e: e.tensor_tensor(out=t1[:, c, :], in0=py.ap[:, :], in1=rD,
                                                                           op=ALU.mult))(),
                     reads=[py, B_rD], writes=[B_t1])
                S.op("act", (lambda c=c: lambda e: e.activation(out=ysq[:, 6 + c, :], in_=t1[:, c, :],
                                                                 func=AF.Square))(), reads=[B_t1], writes=[B_ysq])
                S.op("dve", (lambda c=c: lambda e: e.tensor_tensor(out=ysg[:, 6 + c, :], in0=t1[:, c, :],
                                                                    in1=sg[:, 6 + c, :], op=ALU.mult))(),
                     reads=[B_t1, B_sg], writes=[B_ysg])
            marks.setdefault('p3t0', len(S.ops))
            for gi, (c0_, c1_, wdt) in enumerate(((0, 4, 512.0), (4, 6, 256.0), (6, 8, 256.0))):
                pbk = PB[gi]

                def fss(e, c0_=c0_, c1_=c1_, pbk=pbk):
                    for c in range(c0_, c1_):
                        i_ = e.matmul(pbk.ap[:, :], lhsT=onesh[:, 2, :], rhs=ysq[:, c, :], start=(c == c0_),
                                      stop=(c == c1_ - 1))
                    return i_
                S.op("pe", fss, reads=[B_ysq, B_cn], writes=[pbk])
                S.op("act", (lambda pbk=pbk, wdt=wdt: lambda e: e.activation(out=rsb, in_=pbk.ap[:, :], func=AF.Ln,
                                                                            scale=1.0 / wdt, bias=cst[:, 1:2]))(),
                     reads=[pbk, B_cst], writes=[B_rsb])
                S.op("act", lambda e: e.activation(out=rsb, in_=rsb, func=AF.Exp, scale=-0.5), reads=[], writes=[B_rsb])
                S.op("dve", (lambda c0_=c0_, c1_=c1_: lambda e: e.tensor_tensor(
                    out=ysg[:, c0_:c1_, :], in0=ysg[:, c0_:c1_, :],
                    in1=bass.AP(tensor=rsb.tensor, offset=rsb.offset, ap=[list(rsb.ap[0]), [0, c1_ - c0_], list(rsb.ap[1])]),
                    op=ALU.mult))(), reads=[B_rsb], writes=[B_ysg])
            marks.setdefault('p3t1', len(S.ops))
            for r in range(4):
                i = g * 4 + r
                tsl = slice(r * 128, (r + 1) * 128)
                xr = xres[i % 2]
                S.op("sync", lambda e, i=i, xr=xr: e.dma_start(out=xr, in_=xo_d[128 + i * 128:128 + (i + 1) * 128, :]),
                     writes=[B_xres[i % 2]], sem=f"xres{i % 2}", inc=16)
                o = ob[i % 2]
                pa = [PB[4 + 2 * (i % 2)], PB[5 + 2 * (i % 2)]]

                def fop(e, tsl=tsl, pa=pa):
                    for half in range(2):
                        for c in range(8):
                            i_ = e.matmul(pa[half].ap[:, :], lhsT=ysg[:, c, tsl], rhs=Wo[:, c, half * 512:(half + 1) * 512],
                                          start=(c == 0), stop=(c == 7))
                    return i_
                S.op("pe", fop, reads=[B_ysg, B_Wo], writes=pa)

                def fcomb(e, o=o, xr=xr, pa=pa):
                    for half in range(2):
                        hs = slice(half * 512, (half + 1) * 512)
                        i_ = e.tensor_tensor(out=o[:, hs], in0=pa[half].ap[:, :], in1=xr[:, hs], op=ALU.add)
                    return i_
                S.op("dve", fcomb, reads=pa + [B_xres[i % 2]], writes=[B_ob[i % 2]])
                marks.setdefault('p3t2', len(S.ops))
                fs = st3[:, 32:33]
                fm = st3[:, 33:34]
                fr = st3[:, 34:35]
                S.op("act", lambda e, o=o: e.activation(out=junk3, in_=o, func=AF.Square, accum_out=fs),
                     reads=[B_ob[i % 2]], writes=[B_junk3, B_st3])
                S.seq("act", [
                    lambda e: e.activation(out=fm, in_=fs, func=AF.Ln, scale=1.0 / D, bias=cst[:, 1:2]),
                    lambda e: e.activation(out=fr, in_=fm, func=AF.Exp, scale=-0.5)],
                    reads=[B_st3, B_cst], writes=[B_st3])
                S.op("dve", lambda e, o=o: e.scalar_tensor_tensor(out=o, in0=o, scalar=fr, in1=gfin, op0=ALU.mult,
                                                                  op1=ALU.mult),
                     reads=[B_st3, B_gfin, B_ob[i % 2]], writes=[B_ob[i % 2]])
                S.op("sync", lambda e, i=i, o=o: e.dma_start(out=out_d[i * 128:(i + 1) * 128, :], in_=o),
                     reads=[B_ob[i % 2]], sem=f"ost{i % 2}", inc=16)

        if stop is not None:
            S.ops = S.ops[:marks[stop]]
        final = {}
        for (en_, fn_, d_, sm_, inc_, c_) in S.ops:
            final[sm_] = final.get(sm_, 0) + inc_

        names = sorted(final.keys())
        sems = {nm: es.enter_context(nc.semaphore(nm)) for nm in names}
        ENG = ("pe", "act", "dve", "pool", "sync")
        unit = {sm_ for (en_, fn_, d_, sm_, inc_, c_) in S.ops if inc_ == 1}
        needed = {}
        for name in ENG:
            waited = {}
            for (en, fn, d, sm, inc, c) in S.ops:
                if en != name:
                    continue
                for k_, v_ in d.items():
                    if name == "pe" and k_ == "pe":
                        continue
                    if waited.get(k_, 0) < v_:
                        waited[k_] = v_
                        needed.setdefault(k_, set()).add(v_)
        for k_, v_ in final.items():
            needed.setdefault(k_, set()).add(v_)
        remap = {}
        for k_ in unit:
            vals = sorted(needed.get(k_, ()))
            remap[k_] = {v_: r_ + 1 for r_, v_ in enumerate(vals)}

        def mapv(k_, v_):
            return remap[k_][v_] if k_ in remap else v_

        block = es.enter_context(nc.Block())
        engs = {"pe": block.tensor, "act": block.scalar, "dve": block.vector, "pool": block.gpsimd, "sync": block.sync}
        for name, deco in engs.items():
            def make(name):
                def body(e):
                    waited = {}
                    for (en, fn, d, sm, inc, c) in S.ops:
                        if en != name:
                            continue
                        for k_, v_ in d.items():
                            if name == "pe" and k_ == "pe":
                                continue
                            if waited.get(k_, 0) < v_:
                                e.wait_ge(sems[k_], mapv(k_, v_))
                                waited[k_] = v_
                        inst = fn(e)
                        if sm in remap:
                            if c in remap[sm]:
                                inst.then_inc(sems[sm], 1)
                        else:
                            inst.then_inc(sems[sm], inc)
                    if name in ("pool", "sync"):
                        for k_, v_ in final.items():
                            if waited.get(k_, 0) < v_:
                                e.wait_ge(sems[k_], mapv(k_, v_))
                return body
            deco(make(name))
    return nc


_CACHE = {}


def _host_inputs(x, mem, g_in, w_in, conv_w, conv_b, g_mem, w_mem_kv, g_sb_out, g_conv_out, g_mem_out, w_out, g_final):
    T = x.shape[1]
    TO = T // NCORES
    x2 = np.ascontiguousarray(x[0], dtype=np.float32)
    w = w_in[0]
    t128 = lambda v: np.ascontiguousarray(np.asarray(v, np.float32).reshape(-1, 128).T)
    vecs = np.zeros((128, 32), np.float32)
    vecs[:, 0:8] = t128(g_in[0])
    vecs[:, 8:16] = t128(g_mem[0])
    vecs[:, 16:24] = t128(np.concatenate([g_sb_out[0], g_conv_out[0], g_mem_out[0]]))
    cw = np.asarray(conv_w[0], np.float32)
    for ch in range(2):
        for i in range(3):
            vecs[:, 24 + ch * 3 + i] = cw[i, ch * 128:(ch + 1) * 128]
        vecs[:, 30 + ch] = np.asarray(conv_b[0], np.float32)[ch * 128:(ch + 1) * 128]
    gfin = np.ascontiguousarray(np.broadcast_to(np.asarray(g_final, np.float32)[None, :], (128, D)))
    q = np.arange(128)[:, None]
    kk = np.arange(128)[None, :]
    cb = np.zeros((128, 384), np.float32)
    cb[:, 0:128] = np.eye(128)
    cb[:, 128:256] = np.where(kk >= q, NEG, 0.0)
    cb[:, 256:384] = np.where(kk < q, 1.0, 0.0)
    cb = cb.astype(ml_dtypes.bfloat16)
    wrest = np.ascontiguousarray(w[:, 1536:3584], dtype=np.float32)
    wout = np.ascontiguousarray(w_out[0], dtype=np.float32)
    wmem = np.ascontiguousarray(w_mem_kv[0], dtype=np.float32)
    mem2 = np.ascontiguousarray(mem[0], dtype=np.float32)
    maps = []
    for c in range(NCORES):
        xo = np.zeros((128 + TO, D), np.float32)
        lo = c * TO - 128
        if lo >= 0:
            xo[:] = x2[lo:lo + 128 + TO]
        else:
            xo[128:] = x2[0:TO]
        wqkv = np.ascontiguousarray(
            np.concatenate([w[:, c * 64:(c + 1) * 64], w[:, 512 + c * 64:512 + (c + 1) * 64],
                            w[:, 1024 + c * 64:1024 + (c + 1) * 64]], axis=1), dtype=np.float32)
        maps.append({"x": x2, "xo": xo, "wqkv": wqkv, "wrest": wrest, "wout": wout, "wmem": wmem, "mem": mem2,
                     "vecs": vecs, "gfin": gfin, "cbf": cb})
    return T, maps


def kernel(x, mem, g_in, w_in, conv_w, conv_b, g_mem, w_mem_kv, g_sb_out, g_conv_out, g_mem_out, w_out, g_final):
    args = [np.asarray(a) for a in (x, mem, g_in, w_in, conv_w, conv_b, g_mem, w_mem_kv, g_sb_out, g_conv_out,
                                    g_mem_out, w_out, g_final)]
    T, maps = _host_inputs(*args)
    if T not in _CACHE:
        _CACHE[T] = build(T)
    nc = _CACHE[T]
    res = run_bass_kernel_spmd(nc, maps, core_ids=list(range(NCORES)))
    outs = [np.asarray(res.results[c]["out"], dtype=np.float32) for c in range(NCORES)]
    return np.concatenate(outs, axis=0)[None, :, :]
```
